# Optimizing a Trainium2 kernel written in Bass

```python
import jax, jax.numpy as jnp
from jax import lax
import numpy as np

D_MODEL = 1024
BATCH = 1
SEQ = 16384
DEPTH = 2
DEC_BATCH = 32
DEC_SEQ = 64
PAST_LEN = 1024

CHUNK = 64
D_CONV = D_MODEL // 2
CONV_W = 3
D_RWKV = D_MODEL // 2
HEAD_SIZE = 64
N_HEADS = D_RWKV // HEAD_SIZE
D_DECAY_LORA = 64
D_AAA_LORA = 64
D_GATE_LORA = 128
LN_X_EPS = 64e-5
D_FF = ((8 * D_MODEL // 3 + 255) // 256) * 256
RMS_EPS = 1e-6

OFF_CONV_X = 0
OFF_CONV_B = D_CONV
OFF_CONV_C = 2 * D_CONV
OFF_RWKV = 3 * D_CONV
RWKV_COLS = 3 * D_RWKV + D_DECAY_LORA + D_AAA_LORA + D_GATE_LORA
OFF_GATE = OFF_RWKV + RWKV_COLS
IN_COLS = OFF_GATE + 2 * D_MODEL
R_R = 0
R_K = D_RWKV
R_V = 2 * D_RWKV
R_W = 3 * D_RWKV
R_A = R_W + D_DECAY_LORA
R_G = R_A + D_AAA_LORA

kernel_name = "hybrid_conv_rwkv7_streaming_step"


def _rms_norm(x, g):
    xf = x.astype(jnp.float32)
    y = xf * lax.rsqrt(jnp.mean(xf * xf, axis=-1, keepdims=True) + RMS_EPS)
    return (y * g.astype(jnp.float32)).astype(x.dtype)


def _wkv_scan(s0, r, w, k, v, a, b):
    def step(s, inp):
        r_t, w_t, k_t, v_t, a_t, b_t = inp
        sa = jnp.einsum('bhij,bhj->bhi', s, a_t)
        s = s * w_t[:, :, None, :] + sa[..., None] * b_t[:, :, None, :] + v_t[..., None] * k_t[:, :, None, :]
        y = jnp.einsum('bhij,bhj->bhi', s, r_t)
        return s, y
    xs = tuple(jnp.moveaxis(t, 1, 0) for t in (r, w, k, v, a, b))
    s_fin, ys = lax.scan(step, s0, xs)
    return jnp.moveaxis(ys, 0, 1), s_fin


def _mixer(h, conv_state, shift_state, wkv_state, p):
    bsz, t_len, _ = h.shape
    proj = jnp.einsum('btd,dc->btc', h, p["w_in"])

    xin = proj[..., OFF_CONV_X:OFF_CONV_X + D_CONV]
    bg = proj[..., OFF_CONV_B:OFF_CONV_B + D_CONV]
    cg = proj[..., OFF_CONV_C:OFF_CONV_C + D_CONV]
    u = cg * xin
    u_pad = jnp.concatenate([conv_state.astype(u.dtype), u], axis=1)
    cw = p["conv_w"]
    y_conv = cw[0] * u_pad[:, 0:t_len] + cw[1] * u_pad[:, 1:t_len + 1] + cw[2] * u_pad[:, 2:t_len + 2]
    new_conv = u_pad[:, -(CONV_W - 1):]
    branch_a = jnp.einsum('btc,cd->btd', bg * y_conv, p["w_conv_out"])

    pr = proj[..., OFF_RWKV:OFF_RWKV + RWKV_COLS]
    prev = jnp.concatenate([shift_state.astype(pr.dtype)[:, None], pr[:, :-1]], axis=1)
    ps = pr + (prev - pr) * p["mu_shift"]
    new_shift = pr[:, -1]
    r = ps[..., R_R:R_R + D_RWKV]
    k = ps[..., R_K:R_K + D_RWKV]
    v = ps[..., R_V:R_V + D_RWKV]
    wl = ps[..., R_W:R_W + D_DECAY_LORA]
    al = ps[..., R_A:R_A + D_AAA_LORA]
    gl = ps[..., R_G:R_G + D_GATE_LORA]

    w_raw = (p["w_decay0"] + jnp.tanh(wl) @ p["w_decay2"]).astype(jnp.float32)
    w_raw = -jax.nn.softplus(-w_raw) - 0.5
    decay = jnp.exp(-jnp.exp(w_raw))
    a = jax.nn.sigmoid((p["a0"] + al @ p["a2"]).astype(jnp.float32))
    g = jax.nn.sigmoid(gl) @ p["g2"]

    hs = (bsz, t_len, N_HEADS, HEAD_SIZE)
    rf = r.astype(jnp.float32).reshape(hs)
    vf = v.astype(jnp.float32).reshape(hs)
    kf = k.astype(jnp.float32)
    kk = (kf * p["k_k"].astype(jnp.float32)).reshape(hs)
    kk = kk / jnp.maximum(jnp.sqrt(jnp.sum(kk * kk, axis=-1, keepdims=True)), 1e-12)
    kf = (kf * (1.0 + (a - 1.0) * p["k_a"].astype(jnp.float32))).reshape(hs)
    a_h = a.reshape(hs)
    y, s_fin = _wkv_scan(wkv_state.astype(jnp.float32), rf, decay.reshape(hs), kf, vf, -kk, kk * a_h)

    mu = jnp.mean(y, axis=-1, keepdims=True)
    var = jnp.mean(jnp.square(y - mu), axis=-1, keepdims=True)
    yn = ((y - mu) * lax.rsqrt(var + LN_X_EPS)).reshape(bsz, t_len, D_RWKV)
    yn = yn * p["ln_x_w"].astype(jnp.float32) + p["ln_x_b"].astype(jnp.float32)
    bonus = jnp.sum(rf * kf * p["r_k"].astype(jnp.float32), axis=-1, keepdims=True) * vf
    y_r = (yn + bonus.reshape(bsz, t_len, D_RWKV)).astype(h.dtype)
    branch_b = jnp.einsum('btc,cd->btd', y_r * g, p["w_rwkv_out"])

    gates = jax.nn.sigmoid(proj[..., OFF_GATE:OFF_GATE + 2 * D_MODEL])
    merged = gates[..., :D_MODEL] * branch_a + gates[..., D_MODEL:] * branch_b
    out = jnp.einsum('btd,de->bte', merged, p["w_o"])
    return out, new_conv, new_shift, s_fin


def _layer(x, conv_state, shift_state, wkv_state, p):
    m, new_conv, new_shift, new_wkv = _mixer(_rms_norm(x, p["norm_mix_pre"]), conv_state, shift_state, wkv_state, p)
    x = x + _rms_norm(m, p["norm_mix_post"])
    h = _rms_norm(x, p["norm_ffn_pre"])
    up = jnp.einsum('btd,df->btf', h, p["w_ffn_up"])
    f = jnp.einsum('btf,fd->btd', jax.nn.silu(up[..., :D_FF]) * up[..., D_FF:], p["w_ffn_down"])
    x = x + _rms_norm(f, p["norm_ffn_post"])
    return x, new_conv.astype(x.dtype), new_shift.astype(x.dtype), new_wkv.astype(x.dtype)


def setup_inputs(seed: int = 0) -> dict:
    key = jax.random.key(seed)
    ks = jax.random.split(key, 32)
    nrm = lambda i, shape, s: jax.random.normal(ks[i], shape, jnp.float32) * s
    L = DEPTH
    return {
        "x_prompt": nrm(0, (BATCH, SEQ, D_MODEL), 1.0),
        "x_sample": nrm(1, (DEC_BATCH, DEC_SEQ, D_MODEL), 1.0),
        "state_conv": nrm(2, (L, DEC_BATCH, CONV_W - 1, D_CONV), 1.0),
        "state_shift": nrm(3, (L, DEC_BATCH, RWKV_COLS), 1.0),
        "state_wkv": nrm(4, (L, DEC_BATCH, N_HEADS, HEAD_SIZE, HEAD_SIZE), 0.3),
        "norm_mix_pre": 1.0 + nrm(5, (L, D_MODEL), 0.05),
        "norm_mix_post": 1.0 + nrm(6, (L, D_MODEL), 0.05),
        "w_in": nrm(7, (L, D_MODEL, IN_COLS), D_MODEL ** -0.5),
        "mu_shift": jax.random.uniform(ks[8], (L, RWKV_COLS), jnp.float32, 0.1, 0.9),
        "conv_w": nrm(9, (L, CONV_W, D_CONV), CONV_W ** -0.5),
        "w_decay0": nrm(10, (L, D_RWKV), 0.5) - 0.5,
        "w_decay2": nrm(11, (L, D_DECAY_LORA, D_RWKV), 0.1),
        "a0": nrm(12, (L, D_RWKV), 0.1),
        "a2": nrm(13, (L, D_AAA_LORA, D_RWKV), 0.5 * D_AAA_LORA ** -0.5),
        "g2": nrm(14, (L, D_GATE_LORA, D_RWKV), D_GATE_LORA ** -0.5),
        "k_k": 0.85 + nrm(15, (L, D_RWKV), 0.05),
        "k_a": 1.0 + nrm(16, (L, D_RWKV), 0.05),
        "r_k": nrm(17, (L, N_HEADS, HEAD_SIZE), 0.1),
        "ln_x_w": 1.0 + nrm(18, (L, D_RWKV), 0.05),
        "ln_x_b": nrm(19, (L, D_RWKV), 0.01),
        "w_conv_out": nrm(20, (L, D_CONV, D_MODEL), D_CONV ** -0.5),
        "w_rwkv_out": nrm(21, (L, D_RWKV, D_MODEL), D_RWKV ** -0.5),
        "w_o": nrm(22, (L, D_MODEL, D_MODEL), D_MODEL ** -0.5),
        "norm_ffn_pre": 1.0 + nrm(23, (L, D_MODEL), 0.05),
        "norm_ffn_post": 1.0 + nrm(24, (L, D_MODEL), 0.05),
        "w_ffn_up": nrm(25, (L, D_MODEL, 2 * D_FF), D_MODEL ** -0.5),
        "w_ffn_down": nrm(26, (L, D_FF, D_MODEL), D_FF ** -0.5),
    }


def reference(x_prompt, x_sample, state_conv, state_shift, state_wkv,
              norm_mix_pre, norm_mix_post, w_in, mu_shift, conv_w,
              w_decay0, w_decay2, a0, a2, g2, k_k, k_a, r_k, ln_x_w, ln_x_b,
              w_conv_out, w_rwkv_out, w_o, norm_ffn_pre, norm_ffn_post,
              w_ffn_up, w_ffn_down):
    dt = x_prompt.dtype
    xp, xs = x_prompt, x_sample
    conv_p, shift_p, wkv_p, conv_s, shift_s, wkv_s = [], [], [], [], [], []
    for l in range(DEPTH):
        p = dict(norm_mix_pre=norm_mix_pre[l], norm_mix_post=norm_mix_post[l], w_in=w_in[l],
                 mu_shift=mu_shift[l], conv_w=conv_w[l], w_decay0=w_decay0[l], w_decay2=w_decay2[l],
                 a0=a0[l], a2=a2[l], g2=g2[l], k_k=k_k[l], k_a=k_a[l], r_k=r_k[l],
                 ln_x_w=ln_x_w[l], ln_x_b=ln_x_b[l], w_conv_out=w_conv_out[l],
                 w_rwkv_out=w_rwkv_out[l], w_o=w_o[l], norm_ffn_pre=norm_ffn_pre[l],
                 norm_ffn_post=norm_ffn_post[l], w_ffn_up=w_ffn_up[l], w_ffn_down=w_ffn_down[l])
        xp, c1, s1, k1 = _layer(xp,
                                jnp.zeros((BATCH, CONV_W - 1, D_CONV), dt),
                                jnp.zeros((BATCH, RWKV_COLS), dt),
                                jnp.zeros((BATCH, N_HEADS, HEAD_SIZE, HEAD_SIZE), dt), p)
        xs, c2, s2, k2 = _layer(xs, state_conv[l], state_shift[l], state_wkv[l], p)
        conv_p.append(c1); shift_p.append(s1); wkv_p.append(k1)
        conv_s.append(c2); shift_s.append(s2); wkv_s.append(k2)
    return (xp, xs,
            jnp.stack(conv_p, 0), jnp.stack(shift_p, 0), jnp.stack(wkv_p, 0),
            jnp.stack(conv_s, 0), jnp.stack(shift_s, 0), jnp.stack(wkv_s, 0))
```

```python
import numpy as np
from contextlib import ExitStack
import concourse.bass as bass
import concourse.mybir as mybir
from concourse.bass_utils import run_bass_kernel_spmd

F32 = mybir.dt.float32
F32R = mybir.dt.float32r
AF = mybir.ActivationFunctionType
ALU = mybir.AluOpType

NCORES = 8
D = 1024
LAYERS = 2
PTOK = 2048
NTOK = 2304
TB = 128
NBLK = 18
NPB = 16
NCH = TB // 64
NTT = TB // 128
DFF = 2816
INC = 5376
OFF_RWKV = 1536
OFF_GATE = 3328
PV_NMP, PV_NMO, PV_NFP, PV_NFO, PV_MU, PV_CW, PV_A0, PV_KK, PV_KA, PV_RK = 0, 8, 16, 24, 32, 46, 58, 62, 66, 70
PV_N = 74


class Sched:
    ENG = ["pe", "act", "dve", "pool", "sp"]

    def __init__(self, nc, esems, dsems):
        self.nc = nc
        self.esem = esems
        self.cnt = {e: 0 for e in self.ENG}
        self.prog = {e: [] for e in self.ENG}
        self.dsems = dsems
        self.dcnt = [0] * len(dsems)
        self.dnext = 0
        self.lastw = {}
        self.reads = {}
        self.known = {e: {} for e in self.ENG}

    def _wait(self, eng, tok):
        if tok is None:
            return
        semid, sem, val = tok
        k = self.known[eng]
        if k.get(semid, 0) >= val:
            return
        k[semid] = val
        self.prog[eng].append(I("wait_ge", sem, val))

    def _deps(self, eng, reads, writes):
        for r in reads:
            self._wait(eng, self.lastw.get(r))
        for w in writes:
            self._wait(eng, self.lastw.get(w))
            for t in self.reads.get(w, []):
                self._wait(eng, t)

    def _commit(self, tok, reads, writes):
        for r in reads:
            self.reads.setdefault(r, []).append(tok)
        for w in writes:
            self.lastw[w] = tok
            self.reads[w] = []

    def op(self, eng, fn, reads=(), writes=()):
        self._deps(eng, reads, writes)
        self.cnt[eng] += 1
        val = self.cnt[eng]
        sem = self.esem[eng]
        self.prog[eng].append(lambda e, fn=fn, sem=sem: fn(e).then_inc(sem, 1))
        tok = ("e_" + eng, sem, val)
        self._commit(tok, reads, writes)
        return tok

    def group(self, eng, fns, reads=(), writes=()):
        self._deps(eng, reads, writes)
        self.cnt[eng] += 1
        val = self.cnt[eng]
        sem = self.esem[eng]
        n = len(fns)
        for i, fn in enumerate(fns):
            if i == n - 1:
                self.prog[eng].append(lambda e, fn=fn, sem=sem: fn(e).then_inc(sem, 1))
            else:
                self.prog[eng].append(lambda e, fn=fn: fn(e))
        tok = ("e_" + eng, sem, val)
        self._commit(tok, reads, writes)
        return tok

    def dma(self, eng, fn, reads=(), writes=()):
        self._deps(eng, reads, writes)
        i = self.dnext
        self.dnext = (self.dnext + 1) % len(self.dsems)
        sem = self.dsems[i]
        if self.dcnt[i] > 0:
            self._wait(eng, ("d%d" % i, sem, self.dcnt[i]))
        self.dcnt[i] += 16
        val = self.dcnt[i]
        self.prog[eng].append(lambda e, fn=fn, sem=sem: fn(e).then_inc(sem, 16))
        tok = ("d%d" % i, sem, val)
        self._commit(tok, reads, writes)
        return tok

    def wait_all(self, eng):
        for i, sem in enumerate(self.dsems):
            if self.dcnt[i] > 0:
                self._wait(eng, ("d%d" % i, sem, self.dcnt[i]))
        for e in self.ENG:
            if self.cnt[e] > 0:
                self._wait(eng, ("e_" + e, self.esem[e], self.cnt[e]))

    def emit(self, block):
        P = self.prog

        @block.tensor
        def _(e):
            for f in P["pe"]:
                f(e)

        @block.scalar
        def _(e):
            for f in P["act"]:
                f(e)

        @block.vector
        def _(e):
            for f in P["dve"]:
                f(e)

        @block.gpsimd
        def _(e):
            for f in P["pool"]:
                f(e)

        @block.sync
        def _(e):
            for f in P["sp"]:
                f(e)


def I(name, *a, **kw):
    return lambda e: getattr(e, name)(*a, **kw)


DBG = {}


class _Stop(Exception):
    pass


def ck(n):
    if DBG.get("ck") == n:
        raise _Stop()


def build_program():
    nc = bass.Bass("TRN2", target_bir_lowering=False)
    dbg_blocks = DBG.get("blocks")
    dbg_stop = DBG.get("stop")
    nc.dge_precook = False
    L = LAYERS

    def din(name, shape, dt=F32):
        return nc.dram_tensor(name, shape, dt, kind="ExternalInput").ap()

    def dout(name, shape, dt=F32):
        return nc.dram_tensor(name, shape, dt, kind="ExternalOutput").ap()

    xT = din("xT", [D, NTOK])
    xhalo0 = din("xhalo0", [D, 2])
    pvec = din("pvec", [L, 128, PV_N])
    p64 = din("p64", [L, 64, 16])
    w_in = din("w_in", [L, D, INC], F32R)
    wd2a = din("wd2a", [L, 65, 512])
    a2p = din("a2p", [L, 128, 512])
    g2 = din("g2", [L, 128, 512])
    w_co = din("w_co", [L, 512, D], F32R)
    w_ro = din("w_ro", [L, 512, D], F32R)
    w_o = din("w_o", [L, D, D], F32R)
    w_up = din("w_up", [L, D, 2 * DFF], F32R)
    w_dn = din("w_dn", [L, DFF, D], F32R)
    st_conv = din("st_conv", [L, 128, 4, 4, 2])
    st_shift = din("st_shift", [L, 128, 14, 4])
    st_wkv = din("st_wkv", [L, 64, 4, 8, 64])
    c_ident = din("c_ident", [128, 128])
    c_onesn = din("c_onesn", [128, 128], F32R)
    c_bones = din("c_bones", [128, 128])
    c_id2 = din("c_id2", [128, 64])
    c_maska = din("c_maska", [64, 4, 128])
    c_masksl = din("c_masksl", [64, 4, 64])
    c_tri = din("c_tri", [128, 3, 128])
    c_ones64 = din("c_ones64", [64, 64])
    c_selc = din("c_selc", [128, NCH, 64])
    c_id8 = din("c_id8", [64, 8, 64])
    c_sel = din("c_sel", [128, 8])
    c_lt = din("c_lt", [128, 8])

    yT = dout("yT", [D, NTOK])
    o_conv_s = dout("o_conv_s", [L, 128, 4, 4, 2])
    o_shift_s = dout("o_shift_s", [L, 128, 14, 4])
    o_wkv_s = dout("o_wkv_s", [L, 64, 4, 8, 64])
    o_conv_p = dout("o_conv_p", [L, 128, 4, 2])
    o_shift_p = dout("o_shift_p", [L, 128, 14])
    o_wkv_p = dout("o_wkv_p", [L, 64, 8, 64])

    xs = nc.dram_tensor("xs", [D, NTOK], F32).ap()
    ys = nc.dram_tensor("ys", [64, 8, NTOK], F32).ap()
    qs = nc.dram_tensor("qs", [64, 8, NTOK], F32).ap()
    bs = nc.dram_tensor("bs", [512, NTOK], F32).ap()
    gs = nc.dram_tensor("gs", [512, NTOK], F32).ap()
    ex_in = [nc.dram_tensor("ex_in%d" % l, [64, 1024], F32) for l in range(L)]
    ex_out = [nc.dram_tensor("ex_out%d" % l, [512, 1024], F32) for l in range(L)]
    ex2_in = nc.dram_tensor("ex2_in", [128, 16], F32)
    ex2_out = nc.dram_tensor("ex2_out", [1024, 16], F32)

    with ExitStack() as st:
        def sbp(name, shape, dt=F32):
            return st.enter_context(nc.sbuf_tensor(name, shape, dt))

        ARENA_W = 27200
        ARENA_RW = 14464
        ARENA = st.enter_context(nc.sbuf_tensor("ARENA", [128, ARENA_W], F32))
        ARENAR = st.enter_context(nc.sbuf_tensor("ARENAR", [128, ARENA_RW], F32R))
        aoff = {"A": 0, "B": 0, "X": 0, "AR": 0, "BR": 0, "XR": 0}
        phase = ["P"]

        def sb(name, shape, dt=F32):
            ph = phase[0]
            if ph == "P":
                return sbp(name, shape, dt)
            n = 1
            for d in shape[1:]:
                n *= d
            if dt == F32R:
                ph = ph + "R"
            o = aoff[ph]
            aoff[ph] = o + n
            assert aoff[ph] <= (ARENA_RW if dt == F32R else ARENA_W), (name, ph, aoff[ph])
            ap = (ARENAR if dt == F32R else ARENA)[0:shape[0], o:o + n]
            if len(shape) == 3:
                ap = ap.rearrange("p (a b) -> p a b", a=shape[1])
            elif len(shape) == 4:
                ap = ap.rearrange("p (a b c) -> p a b c", a=shape[1], b=shape[2])
            elif len(shape) == 5:
                ap = ap.rearrange("p (a b c d) -> p a b c d", a=shape[1], b=shape[2], c=shape[3])
            return ap

        esems = {e: st.enter_context(nc.semaphore("s_" + e)) for e in Sched.ENG}
        dsems = [st.enter_context(nc.semaphore("d%d" % i)) for i in range(12)]
        cc_sem = st.enter_context(nc.semaphore("cc"))
        S = Sched(nc, esems, dsems)
        PB = [st.enter_context(nc.psum_tensor("pb%d" % i, [128, 512], F32)) for i in range(8)]

        ident = sb("ident", [128, 128]); onesn = sb("onesn", [128, 128], F32R); bones = sb("bones", [128, 128])
        id2 = sb("id2", [128, 64]); maska = sb("maska", [64, 4, 128]); masksl = sb("masksl", [64, 4, 64])
        tri = sb("tri", [128, 3, 128]); ones64 = sb("ones64", [64, 64]); sel = sb("sel", [128, 8]); lt = sb("lt", [128, 8])
        eps6 = sb("eps6", [128, 1]); epsln = sb("epsln", [128, 1])
        selc = sb("selc", [128, NCH, 64]); id8 = sb("id8", [64, 8, 64])
        for t, d, k in [(ident, c_ident, "ident"), (onesn, c_onesn, "onesn"), (bones, c_bones, "bones"), (id2, c_id2, "id2"),
                        (maska, c_maska, "maska"), (masksl, c_masksl, "masksl"), (tri, c_tri, "tri"), (ones64, c_ones64, "ones64"),
                        (sel, c_sel, "sel"), (lt, c_lt, "lt"), (selc, c_selc, "selc"), (id8, c_id8, "id8")]:
            S.dma("sp", I("dma_start", out=t[:], in_=d), writes=[k])
        S.op("pool", I("memset", eps6[:], 1e-6), writes=["eps6"])
        S.op("pool", I("memset", epsln[:], 64e-5), writes=["epsln"])

        PV = sb("PV", [128, PV_N]); P64 = sb("P64", [64, 16])
        WD2A = sb("WD2A", [65, 512]); A2P = sb("A2P", [128, 512]); G2 = sb("G2", [128, 512])
        XH = sb("XH", [128, 8, 2]); SQH = sb("SQH", [128, 8, 2], F32R); HH = sb("HH", [128, 8, 2], F32R)
        RSH = sb("RSH", [128, 2]); RSTDH = sb("RSTDH", [128, 2])
        LASTPR = sb("LASTPR", [128, 14]); PRHALO = sb("PRHALO", [128, 14, 2]); STSH = sb("STSH", [128, 14, 4])
        LASTU = sb("LASTU", [128, 4, 2]); XINH = sb("XINH", [128, 4, 2]); UHALO = sb("UHALO", [128, 4, 2]); STCV = sb("STCV", [128, 4, 4, 2])
        PSI1 = [sb("PSI_%d" % i, [64, 8, 64]) for i in range(2)]
        PHI1 = [sb("PHI_%d" % i, [64, 8, 64]) for i in range(2)]
        PSI = [PSI1, PSI1]; PHI = [PHI1, PHI1]
        STS = sb("STS", [64, 2, 8, 64]); STO = sb("STO", [64, 2, 8, 64])
        STST = sb("STST", [64, 8, 64]); CAND = sb("CAND", [64, 8, 64]); STFIN = sb("STFIN", [64, 8, 64]); PHIT = sb("PHIT", [64, 8, 64])
        XE = sb("XE", [128, 16]); GA2 = sb("GA2", [128, 8, 16])

        def common():
            ws = [sb("WS%d" % i, [128, 8, 256], F32R) for i in range(2)]
            xb = sb("XB", [128, 8, TB]); sq = sb("SQ", [128, 8, TB], F32R); h = sb("H", [128, 8, TB], F32R)
            r1 = sb("RS1", [128, TB]); r2 = sb("RSTD", [128, TB])
            return ws, xb, sq, h, r1, r2

        phase[0] = "A"
        WS, XB, SQ, H, RS1, RSTD = common()
        PRh = sb("PRh", [128, 14, TB + 4]); PS = sb("PS", [128, 14, TB]); TMP14 = sb("TMP14", [128, 2, TB])
        TW = sb("TW", [65, TB]); SG = sb("SG", [128, TB]); SIGT = sb("SIGT", [128, NTT, 512])
        EI = sb("EI", [128, TB]); EN = sb("EN", [128, TB]); EE = sb("EE", [128, TB]); ER = sb("ER", [128, TB])
        ASIG = sb("ASIG", [128, TB]); GFM = sb("GFM", [128, 4, TB]); BON = sb("BON", [128, 4, TB])
        KK0 = sb("KK0", [128, TB]); KSQ = sb("KSQ", [128, TB]); NRM = sb("NRM", [128, TB]); KK = sb("KK", [128, TB])
        T1 = sb("T1", [128, TB]); KMOD = sb("KMOD", [128, TB]); BV = sb("BV", [128, TB]); RKK = sb("RKK", [128, TB])
        BK = sb("BK", [128, 4, NCH, 2, 64]); AR = sb("AR", [128, 4, NCH, 2, 64]); BKB = sb("BKB", [128, 4, NCH, 2, 64])
        ATt = sb("ATt", [64, 4, NCH, 128]); VTt = sb("VTt", [64, 4, NCH, 128]); BBt = sb("BBt", [64, 4, NCH, 128]); KBt = sb("KBt", [64, 4, NCH, 128])
        AMB = sb("AMB", [64, 8, 128]); AMK = sb("AMK", [64, 8, 128])
        DG = sb("DG", [64, NCH, 8, 64]); GTB = sb("GTB", [64, 512]); RTL = sb("RTL", [64, 8, 64])
        MM = [sb("MM%d" % i, [64, 8, 64]) for i in range(2)]; MT = [sb("MT%d" % i, [64, 8, 64]) for i in range(2)]
        ZZ = [sb("ZZ%d" % i, [64, 8, 128]) for i in range(2)]
        GT = sb("GT", [64, 8, 64]); RP = sb("RP", [64, 8, 64])
        YA = sb("YA", [64, 8, TB]); QQ = sb("QQ", [64, 8, TB])
        phase[0] = "X"
        HB = sb("HB", [64, 8, 128]); GA = sb("GA", [64, 8, 8, 128])
        phase[0] = "B"
        WS_b, XB_b, SQ_b, H_b, RS1_b, RSTD_b = common()
        YAb = sb("YAb", [64, 8, TB]); QQb = sb("QQb", [64, 8, TB])
        YY = sb("YY", [64, 8, TB]); YC = sb("YC", [64, 8, TB]); YSQ = sb("YSQ", [64, 8, TB]); YR = sb("YR", [64, 8, TB])
        BONL = sb("BONL", [64, 8, TB]); GL = sb("GL", [64, 8, TB]); YRG = sb("YRG", [64, 8, TB], F32R)
        XIN = sb("XIN", [128, 4, TB]); BG = sb("BG", [128, 4, TB]); UP = sb("UP", [128, 4, TB + 8])
        YCV = sb("YCV", [128, 4, TB]); YAC = sb("YAC", [128, 4, TB], F32R)
        GATE = sb("GATE", [128, 16, TB]); MA = sb("MA", [128, 8, TB]); MB = sb("MB", [128, TB]); MERGED = sb("MERGED", [128, 8, TB], F32R)
        WCO0 = sb("WCO0", [128, 4, 128], F32R); WRO0 = sb("WRO0", [64, 8, 128], F32R); WCOs = [WCO0, WCO0]; WROs = [WRO0, WRO0]
        WOS = WS
        O1 = sb("O1", [128, 8, TB]); TT = sb("TT", [128, 8, TB])
        SL = sb("SL", [128, TB]); ACTF = sb("ACTF", [128, 22, TB], F32R)
        WDS0 = sb("WDS0", [128, 11, 128], F32R); WDS = [WDS0, WDS0]
        phase[0] = "P"
        print("arena words", aoff)

        def barrier():
            for e_ in Sched.ENG:
                S.wait_all(e_)

        def mm(out, lhsT, rhs, start=True, stop=True):
            return I("matmul", out, lhsT=lhsT, rhs=rhs, start=start, stop=stop)

        wslot = {}

        def load_w(tiles, keyp, src_ap, two=False):
            i = wslot.get(keyp, 0) % 2
            wslot[keyp] = wslot.get(keyp, 0) + 1
            if tiles[0] is tiles[1]:
                i = 0
            t = tiles[i]
            k = "%s%d" % (keyp, i)
            if two:
                S.dma("sp", I("dma_start", out=t[:, :, 0:128], in_=src_ap[0]), writes=[k])
                S.dma("sp", I("dma_start", out=t[:, :, 128:256], in_=src_ap[1]), reads=[k], writes=[k])
                return t, k
            S.dma("sp", I("dma_start", out=t[:], in_=src_ap), writes=[k])
            return t, k

        def rmsnorm(l, pvcol, xsrc, xkey, sq, h, rs1, rstd, n, tag):
            S.op("pool", I("tensor_tensor", out=sq[:], in0=xsrc[:], in1=xsrc[:], op=ALU.mult), reads=[xkey], writes=["SQ" + tag])
            S.group("pe", [mm(PB[0][:, 0:n], onesn[:], sq[:, k, :], k == 0, k == 7) for k in range(8)],
                    reads=["SQ" + tag, "onesn"], writes=["pb0"])
            S.op("act", I("activation", out=rs1[:], in_=PB[0][:, 0:n], func=AF.Sqrt, bias=eps6[:, 0:1], scale=1.0),
                 reads=["pb0", "eps6"], writes=["RS1" + tag])
            S.op("dve", I("reciprocal", out=rstd[:], in_=rs1[:]), reads=["RS1" + tag], writes=["RSTD" + tag])
            for k in range(8):
                S.op("dve", I("scalar_tensor_tensor", out=h[:, k, :], in0=xsrc[:, k, :], scalar=PV[:, pvcol + k:pvcol + k + 1],
                                                                   in1=rstd[:], op0=ALU.mult, op1=ALU.mult),
                     reads=[xkey, "PV", "RSTD" + tag], writes=["H" + tag])

        def postnorm_add(l, pvcol, osrc, okey):
            S.op("pool", I("tensor_tensor", out=SQ[:], in0=osrc[:], in1=osrc[:], op=ALU.mult), reads=[okey], writes=["SQ"])
            S.group("pe", [mm(PB[0][:, 0:TB], onesn[:], SQ[:, k, :], k == 0, k == 7) for k in range(8)],
                    reads=["SQ", "onesn"], writes=["pb0"])
            S.op("act", I("activation", out=RS1[:], in_=PB[0][:, 0:TB], func=AF.Sqrt, bias=eps6[:, 0:1], scale=1.0),
                 reads=["pb0", "eps6"], writes=["RS1"])
            S.op("dve", I("reciprocal", out=RSTD[:], in_=RS1[:]), reads=["RS1"], writes=["RSTD"])
            for k in range(8):
                S.op("dve", I("scalar_tensor_tensor", out=TT[:, k, :], in0=osrc[:, k, :], scalar=PV[:, pvcol + k:pvcol + k + 1],
                                                                   in1=RSTD[:], op0=ALU.mult, op1=ALU.mult),
                     reads=[okey, "PV", "RSTD"], writes=["TT"])
            S.op("pool", I("tensor_tensor", out=XB[:], in0=XB[:], in1=TT[:], op=ALU.add), reads=["XB", "TT"], writes=["XB"])

        def blk_info(b):
            if b < NPB:
                return b * TB, 1, TB
            return PTOK + (b - NPB) * TB, TB // 64, 64

        def main_body():
          for l in range(L):
            xsrc_d = xT if l == 0 else xs
            xdst_d = xs if l == 0 else yT
            S.dma("sp", I("dma_start", out=PV[:], in_=pvec[l]), writes=["PV"])
            S.dma("sp", I("dma_start", out=P64[:], in_=p64[l]), writes=["P64"])
            S.dma("sp", I("dma_start", out=WD2A[:], in_=wd2a[l]), writes=["WD2A"])
            S.dma("sp", I("dma_start", out=A2P[:], in_=a2p[l]), writes=["A2P"])
            S.dma("sp", I("dma_start", out=G2[:], in_=g2[l]), writes=["G2"])
            S.dma("sp", I("dma_start", out=STSH[:], in_=st_shift[l]), writes=["STSH"])
            S.dma("sp", I("dma_start", out=STCV[:], in_=st_conv[l]), writes=["STCV"])
            S.op("pool", I("memset", TW[64:65, :], 1.0), writes=["TWone"])
            if l == 0:
                S.dma("sp", I("dma_start", out=XH[:], in_=xhalo0.rearrange("(k p) t -> p k t", p=128)), writes=["XH"])
            rmsnorm(l, PV_NMP, XH, "XH", SQH, HH, RSH, RSTDH, 2, "h")
            w_l = w_in[l].rearrange("(k p) c -> p k c", p=128)
            cur = 0
            S.op("pool", I("memset", PSI[l][0][:], 0.0), writes=["PSI_0"])
            S.op("pool", I("memset", PHI[l][0][:], 0.0), writes=["PHI_0"])
            for h in range(8):
                S.op("pool", I("tensor_copy", out=PHI[l][0][:, h, :], in_=ident[0:64, 0:64]), reads=["ident"], writes=["PHI_0"])

            for b in range(NBLK):
                if dbg_blocks is not None and b not in dbg_blocks:
                    continue
                t0, nseq, SL_ = blk_info(b)
                prompt = b < NPB
                sq0 = 0 if prompt else (b - NPB) * (TB // 64)
                S.dma("sp", I("dma_start", out=XB[:], in_=xsrc_d[:, t0:t0 + TB].rearrange("(k p) t -> p k t", p=128)), reads=["xdst%d_%d" % (l - 1, b)], writes=["XB"])
                ck(0)
                rmsnorm(l, PV_NMP, XB, "XB", SQ, H, RS1, RSTD, TB, "")
                ck(1)
                PRv = PRh[:, :, 0:nseq * (SL_ + 1)].rearrange("p m (s t) -> p m s t", s=nseq)
                for g in range(7):
                    wt, wk = load_w(WS, "WS", w_l[:, :, OFF_RWKV + g * 256:OFF_RWKV + (g + 1) * 256])
                    for mmi in range(2):
                        m = g * 2 + mmi
                        pb = PB[1 + (m % 2)]
                        S.group("pe", [mm(pb[:, 0:TB], wt[:, k, mmi * 128:(mmi + 1) * 128], H[:, k, :], k == 0, k == 7) for k in range(8)],
                                reads=[wk, "H"], writes=["pb%d" % (1 + m % 2)])
                        S.op("act", I("copy", out=PRv[:, m, :, 1:SL_ + 1], in_=pb[:, 0:TB].rearrange("p (s t) -> p s t", s=nseq)),
                             reads=["pb%d" % (1 + m % 2)], writes=["PRh"])
                        if b == 0:
                            S.group("pe", [mm(PB[3][:, 0:2], wt[:, k, mmi * 128:(mmi + 1) * 128], HH[:, k, :], k == 0, k == 7) for k in range(8)],
                                    reads=[wk, "Hh"], writes=["pb3"])
                            S.op("dve", I("tensor_copy", out=PRHALO[:, m, :], in_=PB[3][:, 0:2]), reads=["pb3"], writes=["PRHALO"])
                ck(2)
                if prompt:
                    if b == 0:
                        S.op("dve", I("tensor_copy", out=PRv[:, :, 0, 0:1], in_=PRHALO[:, :, 1:2]), reads=["PRHALO"], writes=["PRh"])
                    else:
                        S.op("dve", I("tensor_copy", out=PRv[:, :, 0, 0:1], in_=LASTPR[:].unsqueeze(2)), reads=["LASTPR"], writes=["PRh"])
                    S.op("dve", I("tensor_copy", out=LASTPR[:].unsqueeze(2), in_=PRv[:, :, 0, SL_:SL_ + 1]), reads=["PRh"], writes=["LASTPR"])
                    if b == NPB - 1:
                        S.dma("sp", I("dma_start", out=o_shift_p[l], in_=LASTPR[:]), reads=["LASTPR"])
                else:
                    S.op("dve", I("tensor_copy", out=PRv[:, :, :, 0:1], in_=STSH[:, :, sq0:sq0 + nseq].unsqueeze(3)), reads=["STSH"], writes=["PRh"])
                    S.op("dve", I("tensor_copy", out=STSH[:, :, sq0:sq0 + nseq].unsqueeze(3), in_=PRv[:, :, :, SL_:SL_ + 1]), reads=["PRh"], writes=["STSH"])
                    if b == NBLK - 1:
                        S.dma("sp", I("dma_start", out=o_shift_s[l], in_=STSH[:]), reads=["STSH"])
                for m in range(14):
                    eng = "dve" if m % 2 == 0 else "pool"
                    S.op(eng, I("tensor_tensor", out=TMP14[:, m % 2, :].rearrange("p (s t) -> p s t", s=nseq), in0=PRv[:, m, :, 0:SL_],
                                in1=PRv[:, m, :, 1:SL_ + 1], op=ALU.subtract), reads=["PRh"], writes=["TMP14_%d" % (m % 2)])
                    S.op("dve", I("scalar_tensor_tensor", out=PS[:, m, :].rearrange("p (s t) -> p s t", s=nseq),
                                                                    in0=TMP14[:, m % 2, :].rearrange("p (s t) -> p s t", s=nseq),
                                                                    scalar=PV[:, PV_MU + m:PV_MU + m + 1], in1=PRv[:, m, :, 1:SL_ + 1],
                                                                    op0=ALU.mult, op1=ALU.add),
                         reads=["TMP14_%d" % (m % 2), "PRh", "PV"], writes=["PS"])
                ck(3)
                S.op("act", I("activation", out=TW[0:64, :], in_=PS[0:64, 12, :], func=AF.Tanh), reads=["PS"], writes=["TW"])
                S.op("act", I("activation", out=SG[:], in_=PS[:, 13, :], func=AF.Sigmoid), reads=["PS"], writes=["SG"])
                for tt in range(NTT):
                    S.op("pe", mm(PB[1][:, :], TW[0:65, tt * 128:(tt + 1) * 128], WD2A[:]), reads=["TW", "TWone", "WD2A"], writes=["pb1"])
                    S.op("act", I("activation", out=SIGT[:, tt, :], in_=PB[1][:, :], func=AF.Sigmoid), reads=["pb1"], writes=["SIGT"])
                for c in range(NCH):
                    S.op("pe", mm(PB[1][0:64, :], selc[:, c, :], SIGT[:, 0, :]), reads=["selc", "SIGT"], writes=["pb1"])
                    S.op("act", I("activation", out=GTB[:], in_=PB[1][0:64, :], func=AF.Exp), reads=["pb1"], writes=["GTB"])
                    S.op("dve", I("tensor_tensor", out=DG[:, c, :, :], in0=GTB[:].rearrange("p (h f) -> p h f", h=8), in1=id8[:], op=ALU.mult),
                         reads=["GTB", "id8"], writes=["DG"])
                ck(31)
                for cb in range(4):
                    fns = []
                    for q in range(3):
                        for tt in range(NTT):
                            fns.append(mm(PB[4 + q][:, tt * 128:(tt + 1) * 128],
                                          SIGT[:, tt, cb * 128:(cb + 1) * 128], tri[:, q, :]))
                    S.group("pe", fns, reads=["SIGT", "tri"], writes=["pb4", "pb5", "pb6"])
                    S.op("act", I("activation", out=EI[:], in_=PB[4][:, 0:TB], func=AF.Exp), reads=["pb4"], writes=["EI"])
                    S.op("act", I("activation", out=EN[:], in_=PB[4][:, 0:TB], func=AF.Exp, scale=-1.0), reads=["pb4"], writes=["EN"])
                    S.op("act", I("activation", out=EE[:], in_=PB[5][:, 0:TB], func=AF.Exp), reads=["pb5"], writes=["EE"])
                    S.op("act", I("activation", out=ER[:], in_=PB[6][:, 0:TB], func=AF.Exp), reads=["pb6"], writes=["ER"])
                    S.op("pe", mm(PB[1][:, 0:TB], A2P[64:128, cb * 128:(cb + 1) * 128], PS[64:128, 12, :]), reads=["A2P", "PS"], writes=["pb1"])
                    S.op("act", I("activation", out=ASIG[:], in_=PB[1][:, 0:TB], func=AF.Sigmoid,
                                                              bias=PV[:, PV_A0 + cb:PV_A0 + cb + 1], scale=1.0), reads=["pb1", "PV"], writes=["ASIG"])
                    S.op("pe", mm(PB[2][:, 0:TB], G2[:, cb * 128:(cb + 1) * 128], SG[:]), reads=["G2", "SG"], writes=["pb2"])
                    S.op("dve", I("tensor_copy", out=GFM[:, cb, :], in_=PB[2][:, 0:TB]), reads=["pb2"], writes=["GFM"])
                    ck(41)
                    r_ = PS[:, cb, :]; k_ = PS[:, 4 + cb, :]; v_ = PS[:, 8 + cb, :]
                    c4 = lambda ap: ap.rearrange("p (c t) -> p c t", c=NCH)
                    S.op("pool", I("tensor_scalar", out=KK0[:], in0=k_, scalar1=PV[:, PV_KK + cb:PV_KK + cb + 1], scalar2=None, op0=ALU.mult),
                         reads=["PS", "PV"], writes=["KK0"])
                    S.op("pool", I("tensor_tensor", out=KSQ[:], in0=KK0[:], in1=KK0[:], op=ALU.mult), reads=["KK0"], writes=["KSQ"])
                    S.op("pe", mm(PB[1][:, 0:TB], bones[:], KSQ[:]), reads=["bones", "KSQ"], writes=["pb1"])
                    S.op("act", I("activation", out=NRM[:], in_=PB[1][:, 0:TB], func=AF.Sqrt), reads=["pb1"], writes=["NRM"])
                    S.op("dve", I("tensor_scalar", out=NRM[:], in0=NRM[:], scalar1=1e-12, scalar2=None, op0=ALU.max), reads=["NRM"], writes=["NRM"])
                    S.op("dve", I("reciprocal", out=NRM[:], in_=NRM[:]), reads=["NRM"], writes=["NRM"])
                    S.op("dve", I("tensor_tensor", out=KK[:], in0=KK0[:], in1=NRM[:], op=ALU.mult), reads=["KK0", "NRM"], writes=["KK"])
                    S.op("dve", I("tensor_scalar", out=T1[:], in0=ASIG[:], scalar1=-1.0, scalar2=PV[:, PV_KA + cb:PV_KA + cb + 1],
                                                          op0=ALU.add, op1=ALU.mult), reads=["ASIG", "PV"], writes=["T1"])
                    S.op("dve", I("scalar_tensor_tensor", out=KMOD[:], in0=T1[:], scalar=1.0, in1=k_, op0=ALU.add, op1=ALU.mult),
                         reads=["T1", "PS"], writes=["KMOD"])
                    S.op("pool", I("tensor_tensor", out=BV[:], in0=KK[:], in1=ASIG[:], op=ALU.mult), reads=["KK", "ASIG"], writes=["BV"])
                    S.op("dve", I("tensor_tensor", out=BK[:, cb, :, 0, :], in0=c4(BV[:]), in1=c4(EN[:]), op=ALU.mult),
                         reads=["BV", "EN"], writes=["BK"])
                    S.op("pool", I("tensor_tensor", out=BK[:, cb, :, 1, :], in0=c4(KMOD[:]), in1=c4(EN[:]), op=ALU.mult),
                         reads=["KMOD", "EN"], writes=["BK"])
                    S.op("dve", I("scalar_tensor_tensor", out=AR[:, cb, :, 0, :], in0=c4(KK[:]), scalar=-1.0, in1=c4(EE[:]),
                                                                 op0=ALU.mult, op1=ALU.mult), reads=["KK", "EE"], writes=["AR"])
                    S.op("pool", I("tensor_tensor", out=AR[:, cb, :, 1, :], in0=c4(r_), in1=c4(EI[:]), op=ALU.mult),
                         reads=["PS", "EI"], writes=["AR"])
                    S.op("dve", I("tensor_tensor", out=BKB[:, cb, :, 0, :], in0=c4(BV[:]), in1=c4(ER[:]), op=ALU.mult),
                         reads=["BV", "ER"], writes=["BKB"])
                    S.op("pool", I("tensor_tensor", out=BKB[:, cb, :, 1, :], in0=c4(KMOD[:]), in1=c4(ER[:]), op=ALU.mult),
                         reads=["KMOD", "ER"], writes=["BKB"])
                    S.op("dve", I("scalar_tensor_tensor", out=RKK[:], in0=r_, scalar=PV[:, PV_RK + cb:PV_RK + cb + 1], in1=KMOD[:],
                                                                 op0=ALU.mult, op1=ALU.mult), reads=["PS", "PV", "KMOD"], writes=["RKK"])
                    S.op("pe", mm(PB[2][:, 0:TB], bones[:], RKK[:]), reads=["bones", "RKK"], writes=["pb2"])
                    S.op("dve", I("tensor_tensor", out=BON[:, cb, :], in0=PB[2][:, 0:TB], in1=v_, op=ALU.mult), reads=["pb2", "PS"], writes=["BON"])
                    ck(42)
                    vc = c4(v_)
                    S.group("pe", [mm(PB[3][0:64, (c * 2 + q) * 128:(c * 2 + q + 1) * 128],
                                     (AR[:, cb, c, 0, :] if q == 0 else vc[:, c, :]), ident[:]) for c in range(NCH) for q in range(2)],
                            reads=["AR", "PS", "ident"], writes=["pb3"])
                    ck(421)
                    p3v = PB[3][0:64, 0:NCH * 256].rearrange("p (c q f) -> p c q f", c=NCH, q=2)
                    S.op("act", I("copy", out=ATt[:, cb, :, :], in_=p3v[:, :, 0, :]), reads=["pb3"], writes=["ATt"])
                    ck(422)
                    S.op("act", I("copy", out=VTt[:, cb, :, :], in_=p3v[:, :, 1, :]), reads=["pb3"], writes=["VTt"])
                    ck(43)
                    S.group("pe", [mm(PB[7][0:64, (c * 2 + q) * 128:(c * 2 + q + 1) * 128], BKB[:, cb, c, q, :], ident[:])
                                   for c in range(NCH) for q in range(2)], reads=["BKB", "ident"], writes=["pb7"])
                    p7v = PB[7][0:64, 0:NCH * 256].rearrange("p (c q f) -> p c q f", c=NCH, q=2)
                    S.op("act", I("copy", out=BBt[:, cb, :, :], in_=p7v[:, :, 0, :]), reads=["pb7"], writes=["BBt"])
                    S.op("act", I("copy", out=KBt[:, cb, :, :], in_=p7v[:, :, 1, :]), reads=["pb7"], writes=["KBt"])
                S.dma("sp", I("dma_start", out=gs[:, t0:t0 + TB].rearrange("(c p) t -> p c t", p=128), in_=GFM[:]), reads=["GFM"], writes=["gs%d" % b])
                S.dma("sp", I("dma_start", out=bs[:, t0:t0 + TB].rearrange("(c p) t -> p c t", p=128), in_=BON[:]), reads=["BON"], writes=["bs%d" % b])

                ck(5)
                if not prompt:
                    S.dma("sp", I("dma_start", out=STS[:], in_=st_wkv[l][:, sq0:sq0 + nseq]), writes=["STS"])
                for c in range(NCH):
                    hs = [(h, h // 2, (h % 2) * 64) for h in range(8)]
                    fns = []
                    for h, cb, p0 in hs:
                        fns.append(mm(PB[1 + h % 2][0:64, (h // 2) * 128:(h // 2 + 1) * 128], BK[p0:p0 + 64, cb, c, 0, :],
                                      AR[p0:p0 + 64, cb, c, :, :].rearrange("p a b -> p (a b)")))
                        fns.append(mm(PB[3 + h % 2][0:64, (h // 2) * 128:(h // 2 + 1) * 128], BK[p0:p0 + 64, cb, c, 1, :],
                                      AR[p0:p0 + 64, cb, c, :, :].rearrange("p a b -> p (a b)")))
                    S.group("pe", fns, reads=["BK", "AR"], writes=["pb1", "pb2", "pb3", "pb4"])
                    par = lambda t, two: t[:].rearrange("p (a two) f -> p a two f", two=2)[:, :, two, :]
                    for two in range(2):
                        S.op("dve", I("tensor_tensor", out=par(AMB, two), in0=PB[1 + two][0:64, :].rearrange("p (h f) -> p h f", h=4),
                                      in1=maska[0:64, 0:4, :], op=ALU.mult), reads=["pb%d" % (1 + two), "maska"], writes=["AMB%d" % two])
                        S.op("dve", I("tensor_tensor", out=par(AMK, two), in0=PB[3 + two][0:64, :].rearrange("p (h f) -> p h f", h=4),
                                      in1=maska[0:64, 0:4, :], op=ALU.mult), reads=["pb%d" % (3 + two), "maska"], writes=["AMK%d" % two])
                    AMBk = ["AMB0", "AMB1"]; AMKk = ["AMK0", "AMK1"]
                    ck(6)
                    S.group("pe", [mm(PB[5 + h % 2][0:64, (h // 2) * 64:(h // 2 + 1) * 64], AR[p0:p0 + 64, cb, c, 0, :], BK[p0:p0 + 64, cb, c, 0, :])
                                   for h, cb, p0 in hs], reads=["BK", "AR"], writes=["pb5", "pb6"])
                    for two in range(2):
                        S.op("dve", I("tensor_tensor", out=par(MT[0], two), in0=PB[5 + two][0:64, 0:256].rearrange("p (h f) -> p h f", h=4),
                                      in1=masksl[:, 0:4, :], op=ALU.mult), reads=["pb%d" % (5 + two), "masksl"], writes=["MT0"])
                    S.op("pool", I("tensor_copy", out=MM[0][:], in_=AMB[:, :, 0:64]), reads=AMBk, writes=["MM0"])
                    ck(7)
                    S.group("pe", [mm(PB[7][0:64, h * 64:(h + 1) * 64], AMK[:, h, 0:64], VTt[:, cb, c, p0:p0 + 64]) for h, cb, p0 in hs],
                            reads=AMKk + ["VTt"], writes=["pb7"])
                    S.op("act", I("copy", out=ZZ[0][:, :, 64:128], in_=PB[7][0:64, :].rearrange("p (h f) -> p h f", h=8)), reads=["pb7"], writes=["ZZ0b"])
                    S.op("pool", I("tensor_copy", out=ZZ[0][:, :, 0:64].rearrange("p (cb hh) f -> p cb hh f", hh=2), in_=ATt[:, :, c, :].rearrange("p cb (hh f) -> p cb hh f", hh=2)),
                         reads=["ATt"], writes=["ZZ0a"])
                    ck(8)
                    zc = 0
                    mc = 0
                    for lev in range(6):
                        zk = ["ZZ%da" % zc, "ZZ%db" % zc]
                        zn = 1 - zc
                        Zc, Zn = ZZ[zc], ZZ[zn]
                        Mc, MTc = MM[mc], MT[mc]
                        PZ = [PB[5], PB[6]]
                        S.group("pe", [mm(PZ[h // 4][0:64, (h % 4) * 128:(h % 4 + 1) * 128], Mc[:, h, :], Zc[:, h, :]) for h in range(8)],
                                reads=zk + ["MM%d" % mc], writes=["pb5", "pb6"])
                        S.op("dve", I("tensor_tensor", out=Zn[:, 0:4, :], in0=PB[5][0:64, :].rearrange("p (h f) -> p h f", h=4),
                                                                             in1=Zc[:, 0:4, :], op=ALU.add), reads=["pb5"] + zk, writes=["ZZ%da" % zn])
                        S.op("dve", I("tensor_tensor", out=Zn[:, 4:8, :], in0=PB[6][0:64, :].rearrange("p (h f) -> p h f", h=4),
                                                                             in1=Zc[:, 4:8, :], op=ALU.add), reads=["pb6"] + zk, writes=["ZZ%db" % zn])
                        zc = zn
                        if lev < 5:
                            mn = 1 - mc
                            S.group("pe", [mm(PB[3][0:64, h * 64:(h + 1) * 64], MTc[:, h, :], Mc[:, h, :]) for h in range(8)],
                                    reads=["MM%d" % mc, "MT%d" % mc], writes=["pb3"])
                            S.group("pe", [mm(PB[4][0:64, h * 64:(h + 1) * 64], Mc[:, h, :], MTc[:, h, :]) for h in range(8)],
                                    reads=["MM%d" % mc, "MT%d" % mc], writes=["pb4"])
                            S.op("act", I("copy", out=MM[mn][:], in_=PB[3][0:64, :].rearrange("p (h f) -> p h f", h=8)), reads=["pb3"], writes=["MM%d" % mn])
                            S.op("act", I("copy", out=MT[mn][:], in_=PB[4][0:64, :].rearrange("p (h f) -> p h f", h=8)), reads=["pb4"], writes=["MT%d" % mn])
                            mc = mn
                    ck(9)
                    Z6 = ZZ[zc]
                    z6k = ["ZZ%da" % zc, "ZZ%db" % zc]
                    S.group("pe", [mm(PB[3][0:64, h * 64:(h + 1) * 64], Z6[:, h, 0:64], BBt[:, cb, c, p0:p0 + 64]) for h, cb, p0 in hs],
                            reads=z6k + ["BBt"], writes=["pb3"])
                    S.op("dve", I("tensor_tensor", out=GT[:], in0=PB[3][0:64, :].rearrange("p (h f) -> p h f", h=8), in1=DG[:, c, :, :], op=ALU.add),
                         reads=["pb3", "DG"], writes=["GT"])
                    ck(10)
                    S.group("pe", [mm(PB[2][0:64, cb * 64:(cb + 1) * 64], id2[64:128, :], AR[64:128, cb, c, 1, :]) for cb in range(4)],
                            reads=["id2", "AR"], writes=["pb2"])
                    S.op("act", I("copy", out=par(RTL, 1), in_=PB[2][0:64, 0:256].rearrange("p (h f) -> p h f", h=4)), reads=["pb2"], writes=["RTL1"])
                    S.op("pool", I("tensor_copy", out=par(RTL, 0), in_=AR[0:64, :, c, 1, :]), reads=["AR"], writes=["RTL0"])
                    S.group("pe", [mm(PB[4][0:64, h * 64:(h + 1) * 64], Z6[:, h, 0:64], AMB[:, h, 64:128]) for h, cb, p0 in hs],
                            reads=z6k + AMBk, writes=["pb4"])
                    S.op("dve", I("tensor_tensor", out=RP[:], in0=PB[4][0:64, :].rearrange("p (h f) -> p h f", h=8), in1=RTL[:], op=ALU.add),
                         reads=["pb4", "RTL0", "RTL1"], writes=["RP"])
                    ck(11)
                    if prompt:
                        PSIc, PSIn = PSI[l][cur], PSI[l][1 - cur]
                        PHIc, PHIn = PHI[l][cur], PHI[l][1 - cur]
                        psik, psink = "PSI_%d" % cur, "PSI_%d" % (1 - cur)
                        phik, phink = "PHI_%d" % cur, "PHI_%d" % (1 - cur)
                        psi_ap = lambda h: PSIc[:, h, :]
                    else:
                        psik = "STS"
                        psi_ap = lambda h, c=c: STS[:, c, h, :]
                    fns = []
                    for h, cb, p0 in hs:
                        o = PB[5][0:64, h * 64:(h + 1) * 64]
                        fns.append(mm(o, Z6[:, h, 64:128], AMB[:, h, 64:128], True, False))
                        fns.append(mm(o, VTt[:, cb, c, p0:p0 + 64], AMK[:, h, 64:128], False, False))
                        fns.append(mm(o, psi_ap(h), RP[:, h, :], False, True))
                    S.group("pe", fns, reads=z6k + AMBk + AMKk + ["VTt", "RP", psik], writes=["pb5"])
                    S.op("act", I("copy", out=YA[:, :, c * 64:(c + 1) * 64], in_=PB[5][0:64, :].rearrange("p (h f) -> p h f", h=8)),
                         reads=["pb5"], writes=["YA"])
                    if prompt:
                        S.group("pe", [mm(PB[6][0:64, h * 64:(h + 1) * 64], PHIc[:, h, :], RP[:, h, :]) for h in range(8)],
                                reads=[phik, "RP"], writes=["pb6"])
                        S.op("act", I("copy", out=QQ[:, :, c * 64:(c + 1) * 64], in_=PB[6][0:64, :].rearrange("p (h f) -> p h f", h=8)),
                             reads=["pb6"], writes=["QQ"])
                    ck(12)
                    fns = []
                    for h, cb, p0 in hs:
                        o = PB[7][0:64, h * 64:(h + 1) * 64]
                        fns.append(mm(o, BBt[:, cb, c, p0:p0 + 64], Z6[:, h, 64:128], True, False))
                        fns.append(mm(o, KBt[:, cb, c, p0:p0 + 64], VTt[:, cb, c, p0:p0 + 64], False, False))
                        fns.append(mm(o, GT[:, h, :], psi_ap(h), False, True))
                    S.group("pe", fns, reads=z6k + ["BBt", "KBt", "VTt", "GT", psik], writes=["pb7"])
                    if prompt:
                        S.op("act", I("copy", out=PSIn[:], in_=PB[7][0:64, :].rearrange("p (h f) -> p h f", h=8)), reads=["pb7"], writes=[psink])
                        S.group("pe", [mm(PB[3][0:64, h * 64:(h + 1) * 64], GT[:, h, :], PHIc[:, h, :]) for h in range(8)],
                                reads=["GT", phik], writes=["pb3"])
                        S.op("act", I("copy", out=PHIn[:], in_=PB[3][0:64, :].rearrange("p (h f) -> p h f", h=8)), reads=["pb3"], writes=[phink])
                        cur = 1 - cur
                    else:
                        S.op("act", I("copy", out=STO[:, c, :, :], in_=PB[7][0:64, :].rearrange("p (h f) -> p h f", h=8)), reads=["pb7"], writes=["STO"])
                ck(13)
                S.dma("sp", I("dma_start", out=ys[:, :, t0:t0 + TB], in_=YA[:]), reads=["YA"], writes=["ys%d" % b])
                if prompt:
                    S.dma("sp", I("dma_start", out=qs[:, :, t0:t0 + TB], in_=QQ[:]), reads=["QQ"], writes=["qs%d" % b])
                else:
                    S.dma("sp", I("dma_start", out=o_wkv_s[l][:, sq0:sq0 + nseq], in_=STO[:]), reads=["STO"])

            if dbg_stop == "A":
                break
            barrier()
            PSIf, PHIf = PSI[l][cur], PHI[l][cur]
            psifk, phifk = "PSI_%d" % cur, "PHI_%d" % cur
            S.group("pe", [mm(PB[1][0:64, h * 64:(h + 1) * 64], PHIf[:, h, :], ident[0:64, 0:64]) for h in range(8)],
                    reads=[phifk, "ident"], writes=["pb1"])
            S.op("act", I("copy", out=PHIT[:], in_=PB[1][0:64, :].rearrange("p (h f) -> p h f", h=8)), reads=["pb1"], writes=["PHIT"])
            S.op("dve", I("tensor_copy", out=HB[:, :, 64:128], in_=PHIT[:]), reads=["PHIT"], writes=["HB"])
            S.op("dve", I("tensor_copy", out=HB[:, :, 0:64], in_=PSIf[:]), reads=[psifk], writes=["HB"])
            S.dma("sp", I("dma_start", out=ex_in[l].ap(), in_=HB[:].rearrange("p h f -> p (h f)")), reads=["HB"], writes=["ex_in%d" % l])
            S._deps("pool", ["ex_in%d" % l], ["ex_out%d" % l])
            S.prog["pool"].append((lambda ein, eout: (lambda e: e.collective_compute("AllGather", ALU.bypass, replica_groups=[list(range(NCORES))],
                                                                       ins=[ein], outs=[eout]).then_inc(cc_sem, 1)))(ex_in[l].ap().opt(), ex_out[l].ap().opt()))
            ccv = l * 2 + 1 if False else None
            cc_count = getattr(S, "cc_count", 0) + 1
            S.cc_count = cc_count
            S.prog["sp"].append(I("wait_ge", cc_sem, cc_count))
            S.dma("sp", I("dma_start", out=GA[:].rearrange("p r h f -> p r (h f)"), in_=ex_out[l].ap().rearrange("(r p) f -> p r f", p=64)),
                  writes=["GA"])
            S.op("pool", I("memset", STST[:], 0.0), writes=["STST"])
            for r in range(7):
                S.group("pe", [mm(PB[1][0:64, h * 64:(h + 1) * 64], GA[:, r, h, 64:128], STST[:, h, :]) for h in range(8)],
                        reads=["GA", "STST"], writes=["pb1"])
                S.op("dve", I("tensor_tensor", out=CAND[:], in0=PB[1][0:64, :].rearrange("p (h f) -> p h f", h=8), in1=GA[:, r, :, 0:64], op=ALU.add),
                     reads=["pb1", "GA"], writes=["CAND"])
                S.op("dve", I("tensor_tensor", out=CAND[:], in0=CAND[:], in1=STST[:], op=ALU.subtract), reads=["CAND", "STST"], writes=["CAND"])
                S.op("dve", I("scalar_tensor_tensor", out=STST[:], in0=CAND[:], scalar=lt[0:64, r:r + 1], in1=STST[:], op0=ALU.mult, op1=ALU.add),
                     reads=["CAND", "STST", "lt"], writes=["STST"])
            S.group("pe", [mm(PB[1][0:64, h * 64:(h + 1) * 64], PHIT[:, h, :], STST[:, h, :]) for h in range(8)], reads=["PHIT", "STST"], writes=["pb1"])
            S.op("dve", I("tensor_tensor", out=STFIN[:], in0=PB[1][0:64, :].rearrange("p (h f) -> p h f", h=8), in1=PSIf[:], op=ALU.add),
                 reads=["pb1", psifk], writes=["STFIN"])
            S.dma("sp", I("dma_start", out=o_wkv_p[l], in_=STFIN[:]), reads=["STFIN"])

            if dbg_stop == "X":
                break
            barrier()
            wco_l = w_co[l].rearrange("(c p) d -> p c d", p=128)
            wro_l = w_ro[l].rearrange("(h p) d -> p h d", p=64)
            wo_l = w_o[l].rearrange("(k p) c -> p k c", p=128)
            wup_l = w_up[l].rearrange("(k p) (two f c) -> p k two f c", p=128, two=2, c=128)
            wdn_l = w_dn[l].rearrange("(f p) c -> p f c", p=128)
            for b in range(NBLK):
                if dbg_blocks is not None and b not in dbg_blocks:
                    continue
                t0, nseq, SL_ = blk_info(b)
                prompt = b < NPB
                sq0 = 0 if prompt else (b - NPB) * (TB // 64)
                S.dma("sp", I("dma_start", out=YAb[:], in_=ys[:, :, t0:t0 + TB]), reads=["ys%d" % b], writes=["YAb"])
                S.dma("sp", I("dma_start", out=BONL[:], in_=bs[:, t0:t0 + TB].rearrange("(h p) t -> p h t", p=64)), reads=["bs%d" % b], writes=["BONL"])
                S.dma("sp", I("dma_start", out=GL[:], in_=gs[:, t0:t0 + TB].rearrange("(h p) t -> p h t", p=64)), reads=["gs%d" % b], writes=["GL"])
                if prompt:
                    S.dma("sp", I("dma_start", out=QQb[:], in_=qs[:, :, t0:t0 + TB]), reads=["qs%d" % b], writes=["QQb"])
                    for hp in range(4):
                        S.group("pe", [mm(PB[1 + hp][0:64, hh * TB:(hh + 1) * TB], STST[:, 2 * hp + hh, :], QQb[:, 2 * hp + hh, :]) for hh in range(2)],
                                reads=["STST", "QQb"], writes=["pb%d" % (1 + hp)])
                        S.op("dve", I("tensor_tensor", out=YY[:, 2 * hp:2 * hp + 2, :], in0=PB[1 + hp][0:64, 0:2 * TB].rearrange("p (h t) -> p h t", h=2),
                                                                      in1=YAb[:, 2 * hp:2 * hp + 2, :], op=ALU.add), reads=["pb%d" % (1 + hp), "YAb"], writes=["YY"])
                else:
                    S.op("dve", I("tensor_copy", out=YY[:], in_=YAb[:]), reads=["YAb"], writes=["YY"])
                for hp in range(4):
                    S.op("pe", mm(PB[1 + hp][0:64, 0:2 * TB], ones64[:], YY[:, 2 * hp:2 * hp + 2, :].rearrange("p h t -> p (h t)")), reads=["ones64", "YY"], writes=["pb%d" % (1 + hp)])
                    S.op("dve", I("tensor_tensor", out=YC[:, 2 * hp:2 * hp + 2, :], in0=YY[:, 2 * hp:2 * hp + 2, :],
                                                                  in1=PB[1 + hp][0:64, 0:2 * TB].rearrange("p (h t) -> p h t", h=2), op=ALU.subtract),
                         reads=["pb%d" % (1 + hp), "YY"], writes=["YC"])
                S.op("pool", I("tensor_tensor", out=YSQ[:], in0=YC[:], in1=YC[:], op=ALU.mult), reads=["YC"], writes=["YSQ"])
                for hp in range(4):
                    S.op("pe", mm(PB[1 + hp][0:64, 0:2 * TB], ones64[:], YSQ[:, 2 * hp:2 * hp + 2, :].rearrange("p h t -> p (h t)")), reads=["ones64", "YSQ"], writes=["pb%d" % (1 + hp)])
                    S.op("act", I("activation", out=YR[:, 2 * hp:2 * hp + 2, :], in_=PB[1 + hp][0:64, 0:2 * TB].rearrange("p (h t) -> p h t", h=2),
                                                               func=AF.Sqrt, bias=epsln[0:64, 0:1], scale=1.0), reads=["pb%d" % (1 + hp), "epsln"], writes=["YR"])
                S.op("dve", I("reciprocal", out=YR[:], in_=YR[:]), reads=["YR"], writes=["YR"])
                S.op("dve", I("tensor_tensor", out=YC[:], in0=YC[:], in1=YR[:], op=ALU.mult), reads=["YC", "YR"], writes=["YC"])
                for h in range(8):
                    S.op("dve", I("tensor_scalar", out=YC[:, h, :], in0=YC[:, h, :], scalar1=P64[:, h:h + 1], scalar2=P64[:, 8 + h:9 + h],
                                                               op0=ALU.mult, op1=ALU.add), reads=["YC", "P64"], writes=["YC"])
                S.op("pool", I("tensor_tensor", out=YC[:], in0=YC[:], in1=BONL[:], op=ALU.add), reads=["YC", "BONL"], writes=["YC"])
                S.op("dve", I("tensor_tensor", out=YRG[:], in0=YC[:], in1=GL[:], op=ALU.mult), reads=["YC", "GL"], writes=["YRG"])
                S.dma("sp", I("dma_start", out=XB[:], in_=xsrc_d[:, t0:t0 + TB].rearrange("(k p) t -> p k t", p=128)), reads=["xdst%d_%d" % (l - 1, b)], writes=["XB"])
                rmsnorm(l, PV_NMP, XB, "XB", SQ, H, RS1, RSTD, TB, "")
                UPv = UP[:, :, 0:nseq * (SL_ + 2)].rearrange("p c (s t) -> p c s t", s=nseq)
                for g in range(6):
                    wt, wk = load_w(WS, "WS", w_l[:, :, g * 256:(g + 1) * 256])
                    for mmi in range(2):
                        m = g * 2 + mmi
                        cb = m % 4
                        pbi = 1 + (m % 2)
                        pb = PB[pbi]
                        S.group("pe", [mm(pb[:, 0:TB], wt[:, k, mmi * 128:(mmi + 1) * 128], H[:, k, :], k == 0, k == 7) for k in range(8)],
                                reads=[wk, "H"], writes=["pb%d" % pbi])
                        if m < 4:
                            S.op("act", I("copy", out=XIN[:, cb, :], in_=pb[:, 0:TB]), reads=["pb%d" % pbi], writes=["XIN"])
                        elif m < 8:
                            S.op("act", I("copy", out=BG[:, cb, :], in_=pb[:, 0:TB]), reads=["pb%d" % pbi], writes=["BG"])
                        else:
                            S.op("dve", I("tensor_tensor", out=UPv[:, cb, :, 2:SL_ + 2], in0=pb[:, 0:TB].rearrange("p (s t) -> p s t", s=nseq),
                                                                                 in1=XIN[:, cb, :].rearrange("p (s t) -> p s t", s=nseq), op=ALU.mult),
                                 reads=["pb%d" % pbi, "XIN"], writes=["UP"])
                        if b == 0 and (m < 4 or m >= 8):
                            S.group("pe", [mm(PB[3][:, 0:2], wt[:, k, mmi * 128:(mmi + 1) * 128], HH[:, k, :], k == 0, k == 7) for k in range(8)],
                                    reads=[wk, "Hh"], writes=["pb3"])
                            if m < 4:
                                S.op("dve", I("tensor_copy", out=XINH[:, cb, :], in_=PB[3][:, 0:2]), reads=["pb3"], writes=["XINH"])
                            else:
                                S.op("dve", I("tensor_tensor", out=UHALO[:, cb, :], in0=PB[3][:, 0:2], in1=XINH[:, cb, :], op=ALU.mult),
                                     reads=["pb3", "XINH"], writes=["UHALO"])
                if prompt:
                    src = UHALO if b == 0 else LASTU
                    S.op("dve", I("tensor_copy", out=UPv[:, :, 0, 0:2], in_=src[:]), reads=["UHALO", "LASTU"], writes=["UP"])
                    S.op("dve", I("tensor_copy", out=LASTU[:], in_=UPv[:, :, 0, SL_:SL_ + 2]), reads=["UP"], writes=["LASTU"])
                    if b == NPB - 1:
                        S.dma("sp", I("dma_start", out=o_conv_p[l], in_=LASTU[:]), reads=["LASTU"])
                else:
                    S.op("dve", I("tensor_copy", out=UPv[:, :, :, 0:2], in_=STCV[:, :, sq0:sq0 + nseq, :]), reads=["STCV"], writes=["UP"])
                    S.op("dve", I("tensor_copy", out=STCV[:, :, sq0:sq0 + nseq, :], in_=UPv[:, :, :, SL_:SL_ + 2]), reads=["UP"], writes=["STCV"])
                    if b == NBLK - 1:
                        S.dma("sp", I("dma_start", out=o_conv_s[l], in_=STCV[:]), reads=["STCV"])
                for cb in range(4):
                    ycv = YCV[:, cb, :].rearrange("p (s t) -> p s t", s=nseq)
                    S.op("dve", I("tensor_scalar", out=ycv, in0=UPv[:, cb, :, 0:SL_], scalar1=PV[:, PV_CW + cb:PV_CW + cb + 1],
                                                                          scalar2=None, op0=ALU.mult), reads=["UP", "PV"], writes=["YCV"])
                    for kk in (1, 2):
                        S.op("dve", I("scalar_tensor_tensor", out=ycv, in0=UPv[:, cb, :, kk:kk + SL_],
                                                                                             scalar=PV[:, PV_CW + kk * 4 + cb:PV_CW + kk * 4 + cb + 1],
                                                                                             in1=ycv, op0=ALU.mult, op1=ALU.add),
                             reads=["UP", "PV", "YCV"], writes=["YCV"])
                S.op("pool", I("tensor_tensor", out=YAC[:], in0=YCV[:], in1=BG[:], op=ALU.mult), reads=["YCV", "BG"], writes=["YAC"])
                for g in range(8):
                    wt, wk = load_w(WS, "WS", w_l[:, :, OFF_GATE + g * 256:OFF_GATE + (g + 1) * 256])
                    for mmi in range(2):
                        m = g * 2 + mmi
                        pbi = 1 + (m % 2)
                        pb = PB[pbi]
                        S.group("pe", [mm(pb[:, 0:TB], wt[:, k, mmi * 128:(mmi + 1) * 128], H[:, k, :], k == 0, k == 7) for k in range(8)],
                                reads=[wk, "H"], writes=["pb%d" % pbi])
                        S.op("act", I("activation", out=GATE[:, m, :], in_=pb[:, 0:TB], func=AF.Sigmoid), reads=["pb%d" % pbi], writes=["GATE"])
                for m in range(8):
                    WCO, wcok = load_w(WCOs, "WCO", wco_l[:, :, m * 128:(m + 1) * 128])
                    WRO, wrok = load_w(WROs, "WRO", wro_l[:, :, m * 128:(m + 1) * 128])
                    S.group("pe", [mm(PB[3][:, 0:TB], WCO[:, cb, :], YAC[:, cb, :], cb == 0, cb == 3) for cb in range(4)],
                            reads=[wcok, "YAC"], writes=["pb3"])
                    S.op("dve", I("tensor_tensor", out=MA[:, m, :], in0=PB[3][:, 0:TB], in1=GATE[:, m, :], op=ALU.mult), reads=["pb3", "GATE"], writes=["MA"])
                    S.group("pe", [mm(PB[4][:, 0:TB], WRO[:, h, :], YRG[:, h, :], h == 0, h == 7) for h in range(8)],
                            reads=[wrok, "YRG"], writes=["pb4"])
                    S.op("dve", I("tensor_tensor", out=MB[:], in0=PB[4][:, 0:TB], in1=GATE[:, 8 + m, :], op=ALU.mult), reads=["pb4", "GATE"], writes=["MB"])
                    S.op("pool", I("tensor_tensor", out=MERGED[:, m, :], in0=MB[:], in1=MA[:, m, :], op=ALU.add), reads=["MB", "MA"], writes=["MERGED"])
                for g in range(4):
                    wt, wk = load_w(WS, "WS", wo_l[:, :, g * 256:(g + 1) * 256])
                    for mmi in range(2):
                        m = g * 2 + mmi
                        pbi = 1 + (m % 2)
                        S.group("pe", [mm(PB[pbi][:, 0:TB], wt[:, k, mmi * 128:(mmi + 1) * 128], MERGED[:, k, :], k == 0, k == 7) for k in range(8)],
                                reads=[wk, "MERGED"], writes=["pb%d" % pbi])
                        S.op("act", I("copy", out=O1[:, m, :], in_=PB[pbi][:, 0:TB]), reads=["pb%d" % pbi], writes=["O1"])
                postnorm_add(l, PV_NMO, O1, "O1")
                rmsnorm(l, PV_NFP, XB, "XB", SQ, H, RS1, RSTD, TB, "")
                for f in range(22):
                    wt, wk = load_w(WS, "WS", (wup_l[:, :, 0, f, :], wup_l[:, :, 1, f, :]), two=True)
                    S.group("pe", [mm(PB[1][:, 0:TB], wt[:, k, 0:128], H[:, k, :], k == 0, k == 7) for k in range(8)], reads=[wk, "H"], writes=["pb1"])
                    S.group("pe", [mm(PB[2][:, 0:TB], wt[:, k, 128:256], H[:, k, :], k == 0, k == 7) for k in range(8)], reads=[wk, "H"], writes=["pb2"])
                    S.op("act", I("activation", out=SL[:], in_=PB[1][:, 0:TB], func=AF.Silu), reads=["pb1"], writes=["SL"])
                    S.op("dve", I("tensor_tensor", out=ACTF[:, f, :], in0=PB[2][:, 0:TB], in1=SL[:], op=ALU.mult), reads=["pb2", "SL"], writes=["ACTF"])
                for m in range(8):
                    pbi = 3 + (m % 2)
                    for half in range(2):
                        wt, wk = load_w(WDS, "WDS", wdn_l[:, half * 11:(half + 1) * 11, m * 128:(m + 1) * 128])
                        S.group("pe", [mm(PB[pbi][:, 0:TB], wt[:, f, :], ACTF[:, half * 11 + f, :], half == 0 and f == 0, half == 1 and f == 10) for f in range(11)],
                                reads=[wk, "ACTF", "pb%d" % pbi] if half else [wk, "ACTF"], writes=["pb%d" % pbi])
                    S.op("act", I("copy", out=O1[:, m, :], in_=PB[pbi][:, 0:TB]), reads=["pb%d" % pbi], writes=["O1"])
                postnorm_add(l, PV_NFO, O1, "O1")
                S.dma("sp", I("dma_start", out=xdst_d[:, t0:t0 + TB].rearrange("(k p) t -> p k t", p=128), in_=XB[:]), reads=["XB"], writes=["xdst%d_%d" % (l, b)])
                if b == NPB - 1 and l == 0:
                    S.op("dve", I("tensor_copy", out=XE[:].rearrange("p (k t) -> p k t", k=8), in_=XB[:, :, TB - 2:TB]), reads=["XB"], writes=["XE"])

            if dbg_stop == "B":
                break
            barrier()
            if l == 0:
                S.dma("sp", I("dma_start", out=ex2_in.ap(), in_=XE[:]), reads=["XE"], writes=["ex2_in"])
                S._deps("pool", ["ex2_in"], ["ex2_out"])
                S.prog["pool"].append((lambda ein, eout: (lambda e: e.collective_compute("AllGather", ALU.bypass, replica_groups=[list(range(NCORES))],
                                                                      ins=[ein], outs=[eout]).then_inc(cc_sem, 1)))(ex2_in.ap().opt(), ex2_out.ap().opt()))
                S.cc_count += 1
                S.prog["sp"].append(I("wait_ge", cc_sem, S.cc_count))
                S.dma("sp", I("dma_start", out=GA2[:], in_=ex2_out.ap().rearrange("(r p) f -> p r f", p=128)), writes=["GA2"])
                S.op("pool", I("memset", XH[:], 0.0), reads=["XH"], writes=["XH"])
                XHf = XH[:].rearrange("p k t -> p (k t)")
                for r in range(8):
                    S.op("dve", I("scalar_tensor_tensor", out=XHf, in0=GA2[:, r, :], scalar=sel[:, r:r + 1], in1=XHf, op0=ALU.mult, op1=ALU.add),
                         reads=["GA2", "sel", "XH"], writes=["XH"])

        try:
            main_body()
        except _Stop:
            pass
        S.wait_all("sp")
        S.wait_all("pool")
        with nc.Block() as block:
            S.emit(block)
    return nc


_NC_CACHE = {}


def _consts(core):
    ident = np.eye(128, dtype=np.float32)
    onesn = np.full((128, 128), 1.0 / 1024, np.float32)
    bones = np.zeros((128, 128), np.float32)
    bones[:64, :64] = 1
    bones[64:, 64:] = 1
    id2 = np.concatenate([np.eye(64, dtype=np.float32)] * 2, 0)
    p = np.arange(128)[:, None] % 64
    c = np.arange(128)[None, :]
    ma = np.where(c < 64, p < (c % 64), p <= (c % 64)).astype(np.float32)
    maska = np.repeat(ma[:64, None, :], 4, 1)
    t = np.arange(64)[:, None]
    s = np.arange(64)[None, :]
    msl = (s < t).astype(np.float32)
    masksl = np.repeat(msl[:, None, :], 4, 1)
    ss = np.arange(128)[:, None]
    tt = np.arange(128)[None, :]
    same = (ss // 64) == (tt // 64)
    f = -float(np.exp(-0.5))
    tri = np.stack([(same & (ss <= tt)), (same & (ss < tt)), (same & (ss > tt))], 1).astype(np.float32) * f
    ones64 = np.full((64, 64), 1.0 / 64, np.float32)
    sel = np.zeros((128, 8), np.float32)
    if core > 0:
        sel[:, core - 1] = 1
    lt = np.zeros((128, 8), np.float32)
    lt[:, :core] = 1
    selc = np.zeros((128, NCH, 64), np.float32)
    for cc in range(NCH):
        selc[cc * 64:(cc + 1) * 64, cc, :] = f
    id8 = np.repeat(np.eye(64, dtype=np.float32)[:, None, :], 8, 1)
    return dict(c_selc=selc, c_id8=np.ascontiguousarray(id8), c_ident=ident, c_onesn=onesn, c_bones=bones, c_id2=id2, c_maska=np.ascontiguousarray(maska),
                c_masksl=np.ascontiguousarray(masksl), c_tri=np.ascontiguousarray(tri), c_ones64=ones64, c_sel=sel, c_lt=lt)


def _prep(x_prompt, x_sample, state_conv, state_shift, state_wkv,
           norm_mix_pre, norm_mix_post, w_in, mu_shift, conv_w,
           w_decay0, w_decay2, a0, a2, g2, k_k, k_a, r_k, ln_x_w, ln_x_b,
           w_conv_out, w_rwkv_out, w_o, norm_ffn_pre, norm_ffn_post,
           w_ffn_up, w_ffn_down):
    f32 = lambda a: np.ascontiguousarray(np.asarray(a, dtype=np.float32))
    x_prompt, x_sample = f32(x_prompt), f32(x_sample)
    state_conv, state_shift, state_wkv = f32(state_conv), f32(state_shift), f32(state_wkv)
    L = LAYERS

    def fm(v, n):
        return f32(v).reshape(L, n, 128).transpose(0, 2, 1)

    pv = np.zeros((L, 128, PV_N), np.float32)
    pv[:, :, PV_NMP:PV_NMP + 8] = fm(norm_mix_pre, 8)
    pv[:, :, PV_NMO:PV_NMO + 8] = fm(norm_mix_post, 8)
    pv[:, :, PV_NFP:PV_NFP + 8] = fm(norm_ffn_pre, 8)
    pv[:, :, PV_NFO:PV_NFO + 8] = fm(norm_ffn_post, 8)
    pv[:, :, PV_MU:PV_MU + 14] = fm(mu_shift, 14)
    cw = f32(conv_w)
    for k in range(3):
        pv[:, :, PV_CW + 4 * k:PV_CW + 4 * k + 4] = fm(cw[:, k], 4)
    pv[:, :, PV_A0:PV_A0 + 4] = fm(a0, 4)
    pv[:, :, PV_KK:PV_KK + 4] = fm(k_k, 4)
    pv[:, :, PV_KA:PV_KA + 4] = fm(k_a, 4)
    pv[:, :, PV_RK:PV_RK + 4] = fm(f32(r_k).reshape(L, 512), 4)
    p64 = np.zeros((L, 64, 16), np.float32)
    p64[:, :, 0:8] = f32(ln_x_w).reshape(L, 8, 64).transpose(0, 2, 1)
    p64[:, :, 8:16] = f32(ln_x_b).reshape(L, 8, 64).transpose(0, 2, 1)
    wd2a = np.concatenate([f32(w_decay2), f32(w_decay0)[:, None, :]], 1)
    a2p = np.concatenate([np.zeros((L, 64, 512), np.float32), f32(a2)], 1)
    shared = dict(pvec=pv, p64=p64, w_in=f32(w_in), wd2a=f32(wd2a), a2p=f32(a2p), g2=f32(g2), w_co=f32(w_conv_out),
                  w_ro=f32(w_rwkv_out), w_o=f32(w_o), w_up=f32(w_ffn_up), w_dn=f32(w_ffn_down))
    in_maps = []
    for c in range(NCORES):
        xp = x_prompt[0, c * PTOK:(c + 1) * PTOK]
        xsm = x_sample[4 * c:4 * c + 4].reshape(256, D)
        xT = f32(np.concatenate([xp, xsm], 0).T)
        if c == 0:
            xh = np.zeros((D, 2), np.float32)
        else:
            xh = f32(x_prompt[0, c * PTOK - 2:c * PTOK].T)
        sc = state_conv[:, 4 * c:4 * c + 4]
        stc = f32(sc.reshape(L, 4, 2, 4, 128).transpose(0, 4, 3, 1, 2))
        ssh = state_shift[:, 4 * c:4 * c + 4]
        sts = f32(ssh.reshape(L, 4, 14, 128).transpose(0, 3, 2, 1))
        sw = state_wkv[:, 4 * c:4 * c + 4]
        stw = f32(sw.transpose(0, 4, 1, 2, 3))
        m = dict(shared)
        m.update(xT=xT, xhalo0=xh, st_conv=stc, st_shift=sts, st_wkv=stw)
        m.update(_consts(c))
        in_maps.append(m)
    return in_maps


def kernel(**inputs):
    in_maps = _prep(**inputs)
    if "nc" not in _NC_CACHE:
        _NC_CACHE["nc"] = build_program()
    res = run_bass_kernel_spmd(_NC_CACHE["nc"], in_maps, core_ids=list(range(NCORES)))
    return _assemble(res.results)


def _assemble(R):
    L = LAYERS
    y_prompt = np.zeros((1, 16384, D), np.float32)
    y_sample = np.zeros((32, 64, D), np.float32)
    conv_s = np.zeros((L, 32, 2, 512), np.float32)
    shift_s = np.zeros((L, 32, 1792), np.float32)
    wkv_s = np.zeros((L, 32, 8, 64, 64), np.float32)
    for c in range(NCORES):
        yt = np.asarray(R[c]["yT"]).T
        y_prompt[0, c * PTOK:(c + 1) * PTOK] = yt[:PTOK]
        y_sample[4 * c:4 * c + 4] = yt[PTOK:].reshape(4, 64, D)
        oc = np.asarray(R[c]["o_conv_s"])
        conv_s[:, 4 * c:4 * c + 4] = oc.transpose(0, 3, 4, 2, 1).reshape(L, 4, 2, 512)
        osf = np.asarray(R[c]["o_shift_s"])
        shift_s[:, 4 * c:4 * c + 4] = osf.transpose(0, 3, 2, 1).reshape(L, 4, 1792)
        ow = np.asarray(R[c]["o_wkv_s"])
        wkv_s[:, 4 * c:4 * c + 4] = ow.transpose(0, 2, 3, 4, 1)
    last = R[NCORES - 1]
    conv_p = np.asarray(last["o_conv_p"]).transpose(0, 3, 2, 1).reshape(L, 1, 2, 512)
    shift_p = np.asarray(last["o_shift_p"]).transpose(0, 2, 1).reshape(L, 1, 1792)
    wkv_p = np.asarray(last["o_wkv_p"]).transpose(0, 2, 3, 1).reshape(L, 1, 8, 64, 64)
    return (y_prompt, y_sample, np.ascontiguousarray(conv_p), np.ascontiguousarray(shift_p), np.ascontiguousarray(wkv_p),
            conv_s, shift_s, wkv_s)
```
